# Optimizing a Trainium2 kernel written in Bass

```python
import math
import jax, jax.numpy as jnp
from jax import lax
import numpy as np

D_MODEL = 2048
BATCH = 1
SEQ = 16384
DEPTH = 1

N_META = 16
BLOCK_Q = 128
ATT_HEAD_DIM = 64
ATT_V_DIM = 2 * ATT_HEAD_DIM
ATT_WIDTH = D_MODEL // 2
N_ATT_HEADS = ATT_WIDTH // ATT_V_DIM
CONV_WIDTH = D_MODEL // 2
CONV_K = 3
N_BRANCH = 2
QK_WIDTH = N_ATT_HEADS * 2 * ATT_HEAD_DIM
IN_WIDTH = 3 * QK_WIDTH + 3 * CONV_WIDTH
D_FF = ((8 * D_MODEL // 3 + 255) // 256) * 256
RMS_EPS = 1e-6
NEG_INF = -1e30

kernel_name = "hybrid_diffattn_shortconv_macaron"


def rms_norm(x, g):
    xf = x.astype(jnp.float32)
    y = xf * lax.rsqrt(jnp.mean(xf * xf, axis=-1, keepdims=True) + RMS_EPS)
    return (y * g.astype(jnp.float32)).astype(x.dtype)


def swiglu_ffn(h, w_gu, w_down):
    gate, up = jnp.split(h @ w_gu, 2, axis=-1)
    return (jax.nn.silu(gate) * up) @ w_down


def lambda_init_for_layer(layer_idx):
    return 0.8 - 0.6 * math.exp(-0.3 * (layer_idx - 1))


def short_conv_mixer(b_gate, c_gate, u, conv_w):
    v = c_gate * u
    L = v.shape[1]
    vp = jnp.pad(v, ((0, 0), (CONV_K - 1, 0), (0, 0)))
    y = conv_w[0] * vp[:, 0:L]
    for tap in range(1, CONV_K):
        y = y + conv_w[tap] * vp[:, tap:tap + L]
    return b_gate * y


def diff_attention(q, k, v, lam, subln_g, lambda_init):
    Bsz, L = q.shape[0], q.shape[1]
    pf = (BLOCK_Q - N_META % BLOCK_Q) % BLOCK_Q
    pb = (-(L + pf)) % BLOCK_Q
    Lp = L + pf + pb
    nb = Lp // BLOCK_Q
    qp = jnp.pad(q, ((0, 0), (pf, pb), (0, 0), (0, 0), (0, 0)))
    kp = jnp.pad(k, ((0, 0), (pf, pb), (0, 0), (0, 0), (0, 0)))
    vp = jnp.pad(v, ((0, 0), (pf, pb), (0, 0), (0, 0)))
    key_pos = jnp.arange(Lp)
    key_valid = (key_pos >= pf) & (key_pos < pf + L)
    scale = ATT_HEAD_DIM ** -0.5
    qb = qp.reshape(Bsz, nb, BLOCK_Q, N_ATT_HEADS, 2, ATT_HEAD_DIM).transpose(1, 0, 2, 3, 4, 5)

    def one_block(args):
        blk, q_blk = args
        q_pos = blk * BLOCK_Q + jnp.arange(BLOCK_Q)
        s = jnp.einsum('bqhcd,bkhcd->bchqk', q_blk, kp,
                       preferred_element_type=jnp.float32) * scale
        mask = (key_pos[None, :] <= q_pos[:, None]) & key_valid[None, :]
        s = jnp.where(mask, s, NEG_INF)
        p = jax.nn.softmax(s, axis=-1)
        a = p[:, 0] - lam * p[:, 1]
        return jnp.einsum('bhqk,bkhv->bqhv', a.astype(vp.dtype), vp)

    out = lax.map(one_block, (jnp.arange(nb), qb))
    out = out.transpose(1, 0, 2, 3, 4).reshape(Bsz, Lp, N_ATT_HEADS, ATT_V_DIM)[:, pf:pf + L]
    out = rms_norm(out, subln_g) * (1.0 - lambda_init)
    return out.reshape(Bsz, L, ATT_WIDTH)


def setup_inputs(seed: int = 0) -> dict:
    key = jax.random.key(seed)
    ks = jax.random.split(key, 24)
    D, F = D_MODEL, D_FF

    def nrm(k, shape, scale):
        return jax.random.normal(k, shape, jnp.float32) * scale

    def gain(k, shape):
        return 1.0 + 0.02 * jax.random.normal(k, shape, jnp.float32)

    return {
        "x": nrm(ks[0], (BATCH, SEQ, D), 1.0),
        "meta_tokens": nrm(ks[1], (N_META, D), 1.0),
        "norm_ffn1": gain(ks[2], (DEPTH, D)),
        "ffn1_w_gu": nrm(ks[3], (DEPTH, D, 2 * F), D ** -0.5),
        "ffn1_w_down": nrm(ks[4], (DEPTH, F, D), F ** -0.5),
        "norm_mix": gain(ks[5], (DEPTH, D)),
        "w_in": nrm(ks[6], (DEPTH, D, IN_WIDTH), D ** -0.5),
        "conv_w": nrm(ks[7], (DEPTH, CONV_K, CONV_WIDTH), CONV_K ** -0.5),
        "lambda_q1": nrm(ks[8], (DEPTH, ATT_HEAD_DIM), 0.1),
        "lambda_k1": nrm(ks[9], (DEPTH, ATT_HEAD_DIM), 0.1),
        "lambda_q2": nrm(ks[10], (DEPTH, ATT_HEAD_DIM), 0.1),
        "lambda_k2": nrm(ks[11], (DEPTH, ATT_HEAD_DIM), 0.1),
        "subln": gain(ks[12], (DEPTH, ATT_V_DIM)),
        "w_branch": nrm(ks[13], (DEPTH, N_BRANCH, ATT_WIDTH, D), ATT_WIDTH ** -0.5),
        "w_gate": nrm(ks[14], (DEPTH, D, N_BRANCH * D), D ** -0.5),
        "w_out": nrm(ks[15], (DEPTH, D, D), D ** -0.5),
        "norm_ffn2": gain(ks[16], (DEPTH, D)),
        "ffn2_w_gu": nrm(ks[17], (DEPTH, D, 2 * F), D ** -0.5),
        "ffn2_w_down": nrm(ks[18], (DEPTH, F, D), F ** -0.5),
        "norm_final": gain(ks[19], (D,)),
    }


def reference(x, meta_tokens, norm_ffn1, ffn1_w_gu, ffn1_w_down, norm_mix, w_in, conv_w,
              lambda_q1, lambda_k1, lambda_q2, lambda_k2, subln, w_branch, w_gate, w_out,
              norm_ffn2, ffn2_w_gu, ffn2_w_down, norm_final):
    Bsz = x.shape[0]
    meta = jnp.broadcast_to(meta_tokens.astype(x.dtype)[None], (Bsz, N_META, D_MODEL))
    s = jnp.concatenate([meta, x], axis=1)
    L = s.shape[1]
    splits = [QK_WIDTH, 2 * QK_WIDTH, 3 * QK_WIDTH,
              3 * QK_WIDTH + CONV_WIDTH, 3 * QK_WIDTH + 2 * CONV_WIDTH]

    for layer in range(DEPTH):
        lam_init = lambda_init_for_layer(layer + 1)
        h = rms_norm(s, norm_ffn1[layer])
        s = s + 0.5 * swiglu_ffn(h, ffn1_w_gu[layer], ffn1_w_down[layer])

        h = rms_norm(s, norm_mix[layer])
        q, k, v, b_gate, c_gate, u = jnp.split(h @ w_in[layer], splits, axis=-1)
        q = q.reshape(Bsz, L, N_ATT_HEADS, 2, ATT_HEAD_DIM)
        k = k.reshape(Bsz, L, N_ATT_HEADS, 2, ATT_HEAD_DIM)
        v = v.reshape(Bsz, L, N_ATT_HEADS, ATT_V_DIM)
        lam = (jnp.exp(jnp.sum(lambda_q1[layer].astype(jnp.float32) * lambda_k1[layer].astype(jnp.float32)))
               - jnp.exp(jnp.sum(lambda_q2[layer].astype(jnp.float32) * lambda_k2[layer].astype(jnp.float32)))
               + lam_init)
        y_att = diff_attention(q, k, v, lam, subln[layer], lam_init)
        y_conv = short_conv_mixer(b_gate, c_gate, u, conv_w[layer])

        ys = jnp.stack([y_att, y_conv], axis=2)
        z = jnp.einsum('blnw,nwd->blnd', ys, w_branch[layer])
        gates = jax.nn.sigmoid(h @ w_gate[layer]).reshape(Bsz, L, N_BRANCH, D_MODEL)
        merged = jnp.sum(gates * z, axis=2)
        s = s + merged @ w_out[layer]

        h = rms_norm(s, norm_ffn2[layer])
        s = s + 0.5 * swiglu_ffn(h, ffn2_w_gu[layer], ffn2_w_down[layer])

    s = rms_norm(s, norm_final)
    return s[:, N_META:]
```

```python
import math
import numpy as np
import concourse.bass as bass
import concourse.mybir as mybir
from concourse.bass_utils import run_bass_kernel_spmd

F32 = mybir.dt.float32
BF16 = mybir.dt.bfloat16
ALU = mybir.AluOpType
AF = mybir.ActivationFunctionType

NCORE = 8
D = 2048
FF = 5632
KC = D // 128
HC = FF // 128
SEQ = 16384
NMETA = 16
TOWN = SEQ // NCORE
EXT = 16
T = TOWN + EXT
HALF = [(0, 1040), (1040, 2064)]
NTS = [[(0, 16), (16, 512), (528, 512)], [(1040, 512), (1552, 512)]]
TH = 1040
NH = 8
EPS = 1e-6
LAM_INIT = 0.8 - 0.6 * math.exp(-0.3 * 0.0)
SLOT = 6144
NSLOT = 4
SEMCHUNK = 12000

ENGS = ("pe", "act", "dve", "pool", "sp")


class Op:
    __slots__ = ("eng", "fn", "waits", "needed", "sem", "val", "isdma", "inc")

    def __init__(self, eng, fn, waits, isdma=False):
        self.eng, self.fn, self.waits, self.isdma = eng, fn, waits, isdma
        self.needed = False
        self.sem = None
        self.val = 0


class Prog:
    def __init__(self, nc):
        self.nc = nc
        self.ops = {e: [] for e in ENGS}
        self.writer = {}
        self.readers = {}
        self.dsem = {}
        self.barrier_toks = {e: [] for e in ENGS}

    def _deps(self, eng, reads, writes):
        toks = []
        for k in reads:
            w = self.writer.get(k)
            if w is not None:
                toks.append(w)
        for k in writes:
            w = self.writer.get(k)
            if w is not None:
                toks.append(w)
            r = self.readers.get(k)
            if r:
                toks.extend(r.values())
        if self.barrier_toks[eng]:
            toks.extend(self.barrier_toks[eng])
            self.barrier_toks[eng] = []
        out = []
        for t in toks:
            if (not t.isdma) and t.eng == eng and eng == "pe":
                continue
            t.needed = True
            out.append(t)
        return out

    def _record(self, op, rkey, reads, writes):
        for k in reads:
            self.readers.setdefault(k, {})[rkey] = op
        for k in writes:
            self.writer[k] = op
            self.readers[k] = {}

    def add(self, eng, fn, reads=(), writes=()):
        op = Op(eng, fn, self._deps(eng, reads, writes))
        self.ops[eng].append(op)
        self._record(op, eng, reads, writes)
        return op

    def dma(self, eng, fn, semname, reads=(), writes=(), inc=16):
        ent = self.dsem.get(semname)
        if ent is None:
            ent = [self.nc.alloc_semaphore("d_" + semname), 0, None]
            self.dsem[semname] = ent
        waits = self._deps(eng, reads, writes)
        if ent[2] is not None:
            ent[2].needed = True
            waits.append(ent[2])
        op = Op(eng, fn, waits, isdma=True)
        op.inc = inc
        ent[1] += inc
        op.sem, op.val = ent[0], ent[1]
        op.needed = True
        ent[2] = op
        self.ops[eng].append(op)
        self._record(op, "dma:" + semname, reads, writes)
        return op

    def barrier(self):
        toks = []
        for e in ENGS:
            for op in reversed(self.ops[e]):
                if not op.isdma:
                    toks.append(op)
                    break
        for ent in self.dsem.values():
            if ent[2] is not None:
                toks.append(ent[2])
        for e in ENGS:
            self.barrier_toks[e] = list(toks)
        self.writer = {}
        self.readers = {}

    def finalize(self):
        nc = self.nc
        for e in ENGS:
            cnt = 0
            sem = None
            for op in self.ops[e]:
                if op.isdma or not op.needed:
                    continue
                if sem is None or cnt >= SEMCHUNK:
                    self.nsem = getattr(self, "nsem", 0) + 1
                    sem = nc.alloc_semaphore(f"c_{e}_{self.nsem}")
                    cnt = 0
                cnt += 1
                op.sem, op.val = sem, cnt

    def emit(self, e, engine, final_waits=()):
        waited = {}
        for op in self.ops[e]:
            for t in op.waits:
                key = t.sem.num
                if waited.get(key, 0) < t.val:
                    engine.wait_ge(t.sem, t.val)
                    waited[key] = t.val
            if isinstance(op.fn, tuple):
                name, args, kw = op.fn
                ins = getattr(engine, name)(*args, **kw)
            else:
                ins = op.fn(engine)
            if op.isdma:
                ins.then_inc(op.sem, op.inc)
            elif op.needed:
                ins.then_inc(op.sem, 1)
        for t in final_waits:
            key = t.sem.num
            if waited.get(key, 0) < t.val:
                engine.wait_ge(t.sem, t.val)
                waited[key] = t.val


def _kmajor(W, c0, n):
    K = W.shape[0]
    return W[:, c0:c0 + n].reshape(K // 128, 128, n).transpose(1, 0, 2).reshape(128, -1)


def _stage_table():
    off = {}
    pos = 0

    def put(name, n):
        nonlocal pos
        off[name] = (pos, n)
        pos += n

    for j in range(HC):
        put(("ffn", 1, j), 6144)
    for st in range(6):
        put(("qk", st), 2048 * min(3, 16 - 3 * st))
    for st, n in enumerate((384, 384, 256)):
        put(("v", st), 16 * n)
    off["__split__"] = (pos, 0)
    for cc in range(8):
        put(("bcu", cc), 6144)
    for oc in range(KC):
        put(("mix", oc), 6144)
    for st in range(6):
        put(("out", st), 2048 * min(3, 16 - 3 * st))
    for j in range(HC):
        put(("ffn", 2, j), 6144)
    return off, pos


STAGES, WCOLS = _stage_table()
WSPLIT = STAGES["__split__"][0]


def _pack_weights(inp):
    wall = np.empty((128, WCOLS), np.float32)

    def put(name, arr):
        o, n = STAGES[name]
        assert arr.shape == (128, n), (name, arr.shape, n)
        wall[:, o:o + n] = arr

    for f, (gu, dn) in enumerate((("ffn1_w_gu", "ffn1_w_down"), ("ffn2_w_gu", "ffn2_w_down")), 1):
        wgu = inp[gu][0]
        wd = inp[dn][0]
        for j in range(HC):
            put(("ffn", f, j), np.concatenate(
                [_kmajor(wgu, j * 128, 128), _kmajor(wgu, FF + j * 128, 128), wd[j * 128:(j + 1) * 128, :]], axis=1))
    w_in = inp["w_in"][0]
    for st in range(6):
        chunks = range(3 * st, min(16, 3 * st + 3))
        put(("qk", st), np.concatenate([_kmajor(w_in, ch * 128, 128) for ch in chunks], axis=1))
    c0 = 0
    for st, n in enumerate((384, 384, 256)):
        put(("v", st), _kmajor(w_in, 2048 + c0, n))
        c0 += n
    for cc in range(8):
        put(("bcu", cc), np.concatenate(
            [_kmajor(w_in, 3072 + cc * 128, 128), _kmajor(w_in, 4096 + cc * 128, 128),
             _kmajor(w_in, 5120 + cc * 128, 128)], axis=1))
    wg = inp["w_gate"][0]
    wb = inp["w_branch"][0]
    for oc in range(KC):
        put(("mix", oc), np.concatenate(
            [_kmajor(wg, oc * 128, 128), _kmajor(wg, D + oc * 128, 128),
             _kmajor(wb[0], oc * 128, 128), _kmajor(wb[1], oc * 128, 128)], axis=1))
    wo = inp["w_out"][0]
    for st in range(6):
        chunks = range(3 * st, min(16, 3 * st + 3))
        put(("out", st), np.concatenate([_kmajor(wo, ch * 128, 128) for ch in chunks], axis=1))
    return wall


NVEC = 16 * 5 + 24 + 1 + 256


def _pack_vecs(inp):
    v = np.zeros((128, NVEC), np.float32)
    for i, k in enumerate(("norm_ffn1", "norm_mix", "norm_ffn2", "norm_final")):
        g = inp[k].reshape(-1)
        v[:, 16 * i:16 * (i + 1)] = g.reshape(16, 128).T
    cw = inp["conv_w"][0]
    v[:, 80:104] = cw.reshape(3, 8, 128).transpose(2, 0, 1).reshape(128, 24)
    v[:, 104] = inp["subln"][0]
    lam = np.concatenate([inp["lambda_q1"][0], inp["lambda_k1"][0], inp["lambda_q2"][0], inp["lambda_k2"][0]])
    v[:, 105:105 + 256] = np.broadcast_to(lam[None, :], (128, 256))
    return v


class _Stop(Exception):
    pass


def build_nc(part):
    nc = bass.Bass("TRN2", target_bir_lowering=False)
    P = Prog(nc)
    X = mybir.AxisListType.X

    EI, EO = "ExternalInput", "ExternalOutput"
    vecs_d = nc.dram_tensor("vecs", [128, NVEC], F32, kind=EI).ap()
    tri_d = nc.dram_tensor("tri", [128, 128], F32, kind=EI).ap()
    if part == 1:
        xT = nc.dram_tensor("xT", [D, T], F32, kind=EI).ap()
        wall = nc.dram_tensor("wall", [128, WSPLIT], F32, kind=EI).ap()
        wbase = 0
        s1d = nc.dram_tensor("s1d", [D, T], F32, kind=EO).ap()
        snd_qk = nc.dram_tensor("snd_qk", [NH * 256, T], BF16, kind=EO)
        snd_v = nc.dram_tensor("snd_v", [NH * T, 128], BF16, kind=EO)
    elif part == 2:
        qt_in = nc.dram_tensor("qt_in", [128, NCORE * T], BF16, kind=EI).ap()
        kt_in = nc.dram_tensor("kt_in", [128, NCORE * T], BF16, kind=EI).ap()
        vs_in = nc.dram_tensor("vs_in", [128, 129 * 128], BF16, kind=EI).ap()
        snd_y = nc.dram_tensor("snd_y", [128, SEQ], BF16, kind=EO)
    else:
        s1d = nc.dram_tensor("s1d", [D, T], F32, kind=EI).ap()
        yatt_in = nc.dram_tensor("yatt_in", [NH * 128, TOWN], BF16, kind=EI).ap()
        wall = nc.dram_tensor("wall", [128, WCOLS - WSPLIT], F32, kind=EI).ap()
        wbase = WSPLIT
        outT = nc.dram_tensor("outT", [D, TOWN], F32, kind=EO).ap()

    big = nc.alloc_sbuf_tensor("big", [128, 24960], F32)
    big16 = big.bitcast(BF16)
    s_sb = big[:, 0:KC * TH].rearrange("p (k t) -> p k t", k=KC)
    h_sb = big16[:, 2 * KC * TH:3 * KC * TH].rearrange("p (k t) -> p k t", k=KC)
    QT = big16[:, 0:NCORE * T]
    KT = big16[:, NCORE * T:2 * NCORE * T]
    VS = big16[:, 2 * NCORE * T:2 * NCORE * T + 129 * 128].rearrange("p (c v) -> p c v", v=128)

    wbuf = nc.alloc_sbuf_tensor("wbuf", [128, NSLOT * SLOT], BF16)
    mixr = nc.alloc_sbuf_tensor("mixr", [128, 8192], F32)
    mix16 = mixr.bitcast(BF16)
    tmp = nc.alloc_sbuf_tensor("tmp", [128, 8 * 512], F32)
    rstd = nc.alloc_sbuf_tensor("rstd", [128, TH], F32)
    vecs = nc.alloc_sbuf_tensor("vecs_sb", [128, NVEC], F32)
    cst = nc.alloc_sbuf_tensor("cst", [128, 128 * 3], F32)
    cst16 = nc.alloc_sbuf_tensor("cst16", [128, 256], BF16)
    small = nc.alloc_sbuf_tensor("small", [128, 64], F32)
    lamtmp = nc.alloc_sbuf_tensor("lamtmp", [128, 128], F32)
    carry = nc.alloc_sbuf_tensor("carry", [128, 16], F32)

    ps = [nc.alloc_psum_tensor(f"ps{b}", [128, 512], F32) for b in range(8)]

    ones_D = cst[:, 0:128]
    ones_V = cst[:, 128:256]
    tri32 = cst[:, 256:384]
    ones16 = cst16[:, 0:128]
    tri16 = cst16[:, 128:256]

    def tt(i, n=512):
        return tmp[:, i * 512:i * 512 + n]

    def gcol(i, kc):
        return vecs[:, 16 * i + kc:16 * i + kc + 1]

    def mm(out, lhsT, rhs, start, stop, reads, writes):
        P.add("pe", ("matmul", (out,), dict(lhsT=lhsT, rhs=rhs, start=start, stop=stop)), reads, writes)

    def act(out, in_, func, reads, writes, **kw):
        P.add("act", ("activation", (), dict(out=out, in_=in_, func=func, **kw)), reads, writes)

    def dve(name, reads, writes, **kw):
        P.add("dve", (name, (), kw), reads, writes)

    def dma(eng, out, in_, sem, reads, writes):
        P.dma(eng, ("dma_start", (), dict(out=out, in_=in_)), sem, reads, writes)

    try:
        dma("sp", vecs[:, :], vecs_d[:, :], "setup", [], ["vecs"])
        dma("sp", tri32, tri_d[:, :], "setup2", [], ["tri32"])
        dve("memset", [], ["onesD"], ap=ones_D, constant=1.0 / D)
        dve("memset", [], ["onesV"], ap=ones_V, constant=1.0 / 128)
        dve("memset", [], ["ones16"], ap=ones16, constant=1.0)
        dve("tensor_copy", ["tri32"], ["tri16"], out=tri16, in_=tri32)
        dve("memset", [], ["carry"], ap=carry[:, :], constant=0.0)
        lq1, lk1, lq2, lk2 = (vecs[:, 105 + 64 * i:105 + 64 * (i + 1)] for i in range(4))
        dve("tensor_tensor", ["vecs"], ["lt0"], out=lamtmp[:, 0:64], in0=lq1, in1=lk1, op=ALU.mult)
        dve("tensor_tensor", ["vecs"], ["lt1"], out=lamtmp[:, 64:128], in0=lq2, in1=lk2, op=ALU.mult)
        dve("reduce_sum", ["lt0"], ["sm0"], out=small[:, 0:1], in_=lamtmp[:, 0:64], axis=X)
        dve("reduce_sum", ["lt1"], ["sm1"], out=small[:, 1:2], in_=lamtmp[:, 64:128], axis=X)
        act(small[:, 2:4], small[:, 0:2], AF.Exp, ["sm0", "sm1"], ["sm2"])
        dve("scalar_tensor_tensor", ["sm2"], ["neglam"], out=small[:, 4:5], in0=small[:, 3:4], scalar=-LAM_INIT,
            in1=small[:, 2:3], op0=ALU.add, op1=ALU.subtract)
        dve("tensor_scalar", ["vecs"], ["subg"], out=small[:, 5:6], in0=vecs[:, 104:105], scalar1=1.0 - LAM_INIT,
            scalar2=None, op0=ALU.mult)
        neglam = small[:, 4:5]
        epscol = small[:, 8:9]
        dve("memset", [], ["eps"], ap=epscol, constant=EPS)
        subg = small[:, 5:6]

        wstate = {"n": 0}

        def wload(name):
            o, n = STAGES[name]
            slot = wstate["n"] % NSLOT
            wstate["n"] += 1
            view = wbuf[:, slot * SLOT:slot * SLOT + n]
            dma("pool", view, wall[:, o - wbase:o - wbase + n], f"w{slot}", [], [("w", slot)])
            return view, ("w", slot)

        psq = {"i": 0}

        def rmsnorm_stats(nts, h0):
            for ti, (c0, n) in nts:
                lo = c0 - h0
                for kc in range(KC):
                    q = 6 + psq["i"] % 2
                    psq["i"] += 1
                    act(tt(q, n), s_sb[:, kc, lo:lo + n], AF.Square, [("s", kc, ti)], [("t", q)])
                    mm(ps[7][:, 0:n], ones_D, tt(q, n), kc == 0, kc == KC - 1, [("t", q), "onesD"], [("ps", 7)])
                act(rstd[:, lo:lo + n], ps[7][:, 0:n], AF.Sqrt, [("ps", 7), "eps"], [("rstd", ti)], bias=epscol, scale=1.0)
                dve("reciprocal", [("rstd", ti)], [("rstd", ti)], out=rstd[:, lo:lo + n], in_=rstd[:, lo:lo + n])

        def rmsnorm_h(nts, h0, gi):
            rmsnorm_stats(nts, h0)
            for ti, (c0, n) in nts:
                lo = c0 - h0
                for kc in range(KC):
                    dve("scalar_tensor_tensor", [("s", kc, ti), ("rstd", ti), "vecs"], [("h", ti)],
                        out=h_sb[:, kc, lo:lo + n], in0=s_sb[:, kc, lo:lo + n], scalar=gcol(gi, kc), in1=rstd[:, lo:lo + n],
                        op0=ALU.mult, op1=ALU.mult)

        a_sb = mix16[:, 0:2 * TH].rearrange("p (j t) -> p j t", j=2)
        jobctr = {"i": 0}

        def bankset(k):
            i = jobctr["i"] % 2
            jobctr["i"] += 1
            return [3 * i + t for t in range(k)]

        sgc = {"i": 0}

        def ffn(f, nts, h0):
            nt = len(nts)
            wds = []
            for j in range(HC):
                wv, wk = wload(("ffn", f, j))
                wg = wv[:, 0:2048].rearrange("p (k c) -> p k c", k=KC)
                wu = wv[:, 2048:4096].rearrange("p (k c) -> p k c", k=KC)
                if j % 2 == 0:
                    wds = []
                wds.append((wv[:, 4096:6144], wk))
                jj = j % 2
                bg = bankset(nt)
                for kc in range(KC):
                    for x, (ti, (c0, n)) in enumerate(nts):
                        lo = c0 - h0
                        mm(ps[bg[x]][:, 0:n], wg[:, kc, :], h_sb[:, kc, lo:lo + n], kc == 0, kc == KC - 1,
                           [wk, ("h", ti)], [("ps", bg[x])])
                bu = bankset(nt)
                for kc in range(KC):
                    for x, (ti, (c0, n)) in enumerate(nts):
                        lo = c0 - h0
                        mm(ps[bu[x]][:, 0:n], wu[:, kc, :], h_sb[:, kc, lo:lo + n], kc == 0, kc == KC - 1,
                           [wk, ("h", ti)], [("ps", bu[x])])
                for x, (ti, (c0, n)) in enumerate(nts):
                    lo = c0 - h0
                    q = sgc["i"] % 6
                    sgc["i"] += 1
                    act(tt(q, n), ps[bg[x]][:, 0:n], AF.Silu, [("ps", bg[x])], [("t", q)])
                    dve("tensor_tensor", [("ps", bu[x]), ("t", q)], [("a", jj, ti)],
                        out=a_sb[:, jj, lo:lo + n], in0=ps[bu[x]][:, 0:n], in1=tt(q, n), op=ALU.mult)
                if jj == 1:
                    for oc in range(KC):
                        bd = bankset(nt)
                        for j2 in range(2):
                            wd, wdk = wds[j2]
                            for x, (ti, (c0, n)) in enumerate(nts):
                                lo = c0 - h0
                                mm(ps[bd[x]][:, 0:n], wd[:, oc * 128:(oc + 1) * 128], a_sb[:, j2, lo:lo + n], j2 == 0, j2 == 1,
                                   [wdk, ("a", j2, ti)], [("ps", bd[x])])
                        for x, (ti, (c0, n)) in enumerate(nts):
                            lo = c0 - h0
                            dve("scalar_tensor_tensor", [("ps", bd[x]), ("s", oc, ti)], [("s", oc, ti)],
                                out=s_sb[:, oc, lo:lo + n], in0=ps[bd[x]][:, 0:n], scalar=0.5, in1=s_sb[:, oc, lo:lo + n],
                                op0=ALU.mult, op1=ALU.add)

        if part == 1:
            sndqk = snd_qk.ap()
            sndv = snd_v.ap()
            sndv_own = [sndv[h * T:h * T + TOWN, :].rearrange("(p c) v -> p c v", p=128) for h in range(NH)]
            sndv_ext = [sndv[h * T + TOWN:(h + 1) * T, :] for h in range(NH)]
            xv = xT.rearrange("(k p) t -> p k t", p=128)
        if part != 2:
            s1v = s1d.rearrange("(k p) t -> p k t", p=128)

        if part == 1:
            stg = mix16[:, 4096:4096 + 4 * TH].rearrange("p (q t) -> p q t", q=4)
            vstg = mix16[:, 12800:12800 + 2 * 384].rearrange("p (q c) -> p q c", q=2)
            sq_i = 0
            vq = 0
            for hf in range(2):
                h0, h1 = HALF[hf]
                hn = h1 - h0
                nts = list(enumerate(NTS[hf]))
                for g in range(4):
                    dma("sp", s_sb[:, 4 * g:4 * g + 4, 0:hn], xv[:, 4 * g:4 * g + 4, h0:h1], f"x{g}", [],
                        [("s", kc, ti) for kc in range(4 * g, 4 * g + 4) for ti, _ in nts])
                rmsnorm_h(nts, h0, 0)
                ffn(1, nts, h0)
                for g in range(4):
                    dma("sp", s1v[:, 4 * g:4 * g + 4, h0:h1], s_sb[:, 4 * g:4 * g + 4, 0:hn], f"sp{g}",
                        [("s", kc, ti) for kc in range(4 * g, 4 * g + 4) for ti, _ in nts], [("s1d", hf, g)])
                rmsnorm_h(nts, h0, 1)
                for st in range(6):
                    wv, wk = wload(("qk", st))
                    for ci, ch in enumerate(range(3 * st, min(16, 3 * st + 3))):
                        wc = wv[:, ci * 2048:(ci + 1) * 2048].rearrange("p (k c) -> p k c", k=KC)
                        bq = bankset(len(nts))
                        for kc in range(KC):
                            for x, (ti, (c0, n)) in enumerate(nts):
                                lo = c0 - h0
                                mm(ps[bq[x]][:, 0:n], wc[:, kc, :], h_sb[:, kc, lo:lo + n], kc == 0, kc == KC - 1,
                                   [wk, ("h", ti)], [("ps", bq[x])])
                        q = sq_i % 4
                        sq_i += 1
                        for x, (ti, (c0, n)) in enumerate(nts):
                            lo = c0 - h0
                            P.add("act", ("mul", (), dict(out=stg[:, q, lo:lo + n], in_=ps[bq[x]][:, 0:n], mul=(0.125 if ch < 8 else 1.0))),
                                  [("ps", bq[x])], [("stg", q)])
                        head, isk = ch % 8, ch // 8
                        r0 = head * 256 + isk * 128
                        dma("sp", sndqk[r0:r0 + 128, h0:h1], stg[:, q, 0:hn], f"sq{q}", [("stg", q)], [("sndqk", ch, hf)])
                tchunks = []
                if hf == 0:
                    tchunks.append((0, 16))
                    tchunks += [(16 + 128 * i, 128) for i in range(8)]
                else:
                    tchunks += [(h0 + 128 * i, 128) for i in range(8)]
                hcol0 = 0
                for st, ncol in enumerate((384, 384, 256)):
                    wv, wk = wload(("v", st))
                    wvv = wv[:, 0:16 * ncol].rearrange("p (k c) -> p k c", k=KC)
                    nhead = ncol // 128
                    for (t0, tn) in tchunks:
                        b = bankset(1)[0]
                        ti = [i for i, (c0, n) in nts if c0 <= t0 < c0 + n][0]
                        lo = t0 - h0
                        for kc in range(KC):
                            mm(ps[b][0:tn, 0:ncol], h_sb[:, kc, lo:lo + tn], wvv[:, kc, :], kc == 0, kc == KC - 1,
                               [wk, ("h", ti)], [("ps", b)])
                        q = vq % 2
                        vq += 1
                        P.add("act", ("copy", (), dict(out=vstg[0:tn, q, 0:ncol], in_=ps[b][0:tn, 0:ncol])), [("ps", b)], [("vstg", q)])
                        for hh in range(nhead):
                            head = hcol0 + hh
                            if tn == 16:
                                dst = sndv_ext[head]
                            else:
                                dst = sndv_own[head][:, (t0 - EXT) // 128, :]
                            dma("sp", dst, vstg[0:tn, q, hh * 128:(hh + 1) * 128], f"sv{q}", [("vstg", q)], [("sndv", head, t0)])
                    hcol0 += nhead
                P.barrier()


        if part == 2:
            dma("sp", QT, qt_in[:, :], "g0", [], ["QT"])
            dma("sp", KT, kt_in[:, :], "g1", [], ["KT"])
            dma("sp", big16[:, 2 * NCORE * T:2 * NCORE * T + 129 * 128], vs_in[:, :], "g2", [], ["VS"])
            E = mix16[:, 0:4 * 512].rearrange("p (b q) -> p b q", b=4)
            ysb = mix16[:, 2048:2048 + 2 * 512].rearrange("p (b q) -> p b q", b=2)
            sndy = snd_y.ap()
            BO = (4, 5)
            BL = (6, 7)

            def kcols(kc):
                return (kc // 16) * T + EXT + (kc % 16) * 128

            for qt in range(32):
                qc0 = (qt // 4) * T + EXT + (qt % 4) * 512
                chunks = [("meta", 0, 16, 0, 0)]
                chunks += [("full", kcols(kc), 128, 1 + kc, 0) for kc in range(4 * qt)]
                chunks += [("diag", kcols(4 * qt + i), 128, 1 + 4 * qt + i, 128 * i) for i in range(4)]
                nchunk = len(chunks)

                def qk(idx, chunks=chunks, qc0=qc0):
                    kind, kcol, nk, vi, q0 = chunks[idx]
                    for c in range(2):
                        b = 2 * (idx % 2) + c
                        mm(ps[b][0:nk, q0:512], KT[64 * c:64 * c + 64, kcol:kcol + nk], QT[64 * c:64 * c + 64, qc0 + q0:qc0 + 512],
                           True, True, ["KT", "QT"], [("ps", b)])

                qk(0)
                for idx in range(nchunk):
                    kind, kcol, nk, vi, q0 = chunks[idx]
                    if idx + 1 < nchunk:
                        qk(idx + 1)
                    for c in range(2):
                        b = 2 * (idx % 2) + c
                        act(E[0:nk, b, q0:512], ps[b][0:nk, q0:512], AF.Exp, [("ps", b)], [("E", b)])
                        if kind == "diag":
                            dve("tensor_tensor", [("E", b), "tri16"], [("E", b)], out=E[:, b, q0:q0 + 128], in0=E[:, b, q0:q0 + 128],
                                in1=tri16, op=ALU.mult)
                        mm(ps[BO[c]][:, q0:512], VS[0:nk, vi, :], E[0:nk, b, q0:512], idx == 0, idx == nchunk - 1,
                           ["VS", ("E", b)], [("ps", BO[c])])
                        mm(ps[BL[c]][:, q0:512], ones16[0:nk, :], E[0:nk, b, q0:512], idx == 0, idx == nchunk - 1,
                           ["ones16", ("E", b)], [("ps", BL[c])])
                for c in range(2):
                    dve("reciprocal", [("ps", BL[c])], [("t", c)], out=tt(c), in_=ps[BL[c]][:, :])
                    dve("tensor_tensor", [("ps", BO[c]), ("t", c)], [("t", 2 + c)], out=tt(2 + c), in0=ps[BO[c]][:, :], in1=tt(c), op=ALU.mult)
                dve("scalar_tensor_tensor", [("t", 2), ("t", 3), "neglam"], [("t", 4)], out=tt(4), in0=tt(3), scalar=neglam, in1=tt(2),
                    op0=ALU.mult, op1=ALU.add)
                act(tt(5), tt(4), AF.Square, [("t", 4)], [("t", 5)])
                mm(ps[BL[0]][:, :], ones_V, tt(5), True, True, [("t", 5), "onesV"], [("ps", BL[0])])
                act(tt(6), ps[BL[0]][:, :], AF.Sqrt, [("ps", BL[0]), "eps"], [("t", 6)], bias=epscol, scale=1.0)
                dve("reciprocal", [("t", 6)], [("t", 6)], out=tt(6), in_=tt(6))
                yq = qt % 2
                dve("scalar_tensor_tensor", [("t", 4), ("t", 6), "subg"], [("ysb", yq)], out=ysb[:, yq, :], in0=tt(4), scalar=subg, in1=tt(6),
                    op0=ALU.mult, op1=ALU.mult)
                dma("sp", sndy[:, qt * 512:(qt + 1) * 512], ysb[:, yq, :], f"y{yq}", [("ysb", yq)], [("sndy", qt)])
            P.barrier()


        if part == 3:
            ycv = mix16[:, 0:8 * 512].rearrange("p (k t) -> p k t", k=8)
            yat = mix16[:, 4096:4096 + 8 * 512].rearrange("p (k t) -> p k t", k=8)
            mrg = mix16[:, 8192:8192 + 16 * 512].rearrange("p (k t) -> p k t", k=KC)
            oq = 0
            for hf in range(2):
                h0, h1 = HALF[hf]
                hn = h1 - h0
                nts = list(enumerate(NTS[hf]))
                for g in range(4):
                    dma("sp", s_sb[:, 4 * g:4 * g + 4, 0:hn], s1v[:, 4 * g:4 * g + 4, h0:h1], f"x{g}", [],
                        [("s", kc, ti) for kc in range(4 * g, 4 * g + 4) for ti, _ in nts])
                rmsnorm_h(nts, h0, 1)
                for ti, (c0, n) in nts:
                    lo = c0 - h0
                    isext = (hf == 0 and ti == 0)
                    for cc in range(8):
                        wv, wk = wload(("bcu", cc))
                        bb = [0, 1, 2] if cc % 2 == 0 else [3, 4, 5]
                        for m in range(3):
                            wc = wv[:, m * 2048:(m + 1) * 2048].rearrange("p (k c) -> p k c", k=KC)
                            for kc in range(KC):
                                mm(ps[bb[m]][:, 0:n], wc[:, kc, :], h_sb[:, kc, lo:lo + n], kc == 0, kc == KC - 1,
                                   [wk, ("h", ti)], [("ps", bb[m])])
                        q = cc % 2
                        csb = tt(q, n)
                        vb = tmp[:, (2 + 2 * q) * 512:(2 + 2 * q) * 512 + n + 2]
                        vkeys = [("t", 2 + 2 * q), ("t", 3 + 2 * q)]
                        yb = tt(6 + q, n)
                        w0 = vecs[:, 80 + cc:80 + cc + 1]
                        w1 = vecs[:, 88 + cc:88 + cc + 1]
                        w2 = vecs[:, 96 + cc:96 + cc + 1]
                        P.add("act", ("copy", (), dict(out=csb, in_=ps[bb[1]][:, 0:n])), [("ps", bb[1])], [("t", q)])
                        dve("tensor_copy", ["carry"], vkeys, out=vb[:, 0:2], in_=carry[:, 2 * cc:2 * cc + 2])
                        dve("tensor_tensor", [("ps", bb[2]), ("t", q)], vkeys, out=vb[:, 2:2 + n], in0=ps[bb[2]][:, 0:n], in1=csb, op=ALU.mult)
                        dve("tensor_scalar", vkeys + ["vecs"], [("t", 6 + q)], out=yb, in0=vb[:, 2:2 + n], scalar1=w2, scalar2=None, op0=ALU.mult)
                        dve("scalar_tensor_tensor", vkeys + ["vecs"], [("t", 6 + q)], out=yb, in0=vb[:, 1:1 + n], scalar=w1, in1=yb,
                            op0=ALU.mult, op1=ALU.add)
                        dve("scalar_tensor_tensor", vkeys + ["vecs"], [("t", 6 + q)], out=yb, in0=vb[:, 0:n], scalar=w0, in1=yb,
                            op0=ALU.mult, op1=ALU.add)
                        dve("tensor_copy", vkeys, ["carry"], out=carry[:, 2 * cc:2 * cc + 2], in_=vb[:, n:n + 2])
                        dve("tensor_tensor", [("ps", bb[0]), ("t", 6 + q)], [("ycv", cc)], out=ycv[:, cc, 0:n], in0=ps[bb[0]][:, 0:n], in1=yb, op=ALU.mult)
                    if isext:
                        continue
                    own0 = c0 - EXT
                    dma("sp", yat[:, :, 0:n], yatt_in.rearrange("(k p) t -> p k t", p=128)[:, :, own0:own0 + n], "ya", [], ["yat"])
                    for oc in range(KC):
                        wv, wk = wload(("mix", oc))
                        wg0 = wv[:, 0:2048].rearrange("p (k c) -> p k c", k=KC)
                        wg1 = wv[:, 2048:4096].rearrange("p (k c) -> p k c", k=KC)
                        wb0 = wv[:, 4096:5120].rearrange("p (k c) -> p k c", k=8)
                        wb1 = wv[:, 5120:6144].rearrange("p (k c) -> p k c", k=8)
                        bb = [0, 1, 2, 3] if oc % 2 == 0 else [4, 5, 6, 7]
                        for kc in range(KC):
                            mm(ps[bb[0]][:, 0:n], wg0[:, kc, :], h_sb[:, kc, lo:lo + n], kc == 0, kc == KC - 1, [wk, ("h", ti)], [("ps", bb[0])])
                        for kc in range(8):
                            mm(ps[bb[1]][:, 0:n], wb0[:, kc, :], yat[:, kc, 0:n], kc == 0, kc == 7, [wk, "yat"], [("ps", bb[1])])
                        for kc in range(KC):
                            mm(ps[bb[2]][:, 0:n], wg1[:, kc, :], h_sb[:, kc, lo:lo + n], kc == 0, kc == KC - 1, [wk, ("h", ti)], [("ps", bb[2])])
                        for kc in range(8):
                            mm(ps[bb[3]][:, 0:n], wb1[:, kc, :], ycv[:, kc, 0:n], kc == 0, kc == 7,
                               [wk] + [("ycv", k2) for k2 in range(8)], [("ps", bb[3])])
                        q = oc % 2
                        g0, g1, m0, m1 = tt(q, n), tt(2 + q, n), tt(4 + q, n), tt(6 + q, n)
                        act(g0, ps[bb[0]][:, 0:n], AF.Sigmoid, [("ps", bb[0])], [("t", q)])
                        act(g1, ps[bb[2]][:, 0:n], AF.Sigmoid, [("ps", bb[2])], [("t", 2 + q)])
                        dve("tensor_tensor", [("ps", bb[1]), ("t", q)], [("t", 4 + q)], out=m0, in0=ps[bb[1]][:, 0:n], in1=g0, op=ALU.mult)
                        dve("tensor_tensor", [("ps", bb[3]), ("t", 2 + q)], [("t", 6 + q)], out=m1, in0=ps[bb[3]][:, 0:n], in1=g1, op=ALU.mult)
                        dve("tensor_tensor", [("t", 4 + q), ("t", 6 + q)], [("mrg", oc)], out=mrg[:, oc, 0:n], in0=m0, in1=m1, op=ALU.add)
                    for st in range(6):
                        wv, wk = wload(("out", st))
                        for ci, oc2 in enumerate(range(3 * st, min(16, 3 * st + 3))):
                            wc = wv[:, ci * 2048:(ci + 1) * 2048].rearrange("p (k c) -> p k c", k=KC)
                            b = oc2 % 8
                            for kc in range(KC):
                                mm(ps[b][:, 0:n], wc[:, kc, :], mrg[:, kc, 0:n], kc == 0, kc == KC - 1, [wk, ("mrg", kc)], [("ps", b)])
                            dve("tensor_tensor", [("ps", b), ("s", oc2, ti)], [("s", oc2, ti)], out=s_sb[:, oc2, lo:lo + n],
                                in0=ps[b][:, 0:n], in1=s_sb[:, oc2, lo:lo + n], op=ALU.add)
                P.barrier()
                rnts = [(ti, (c0, n)) for ti, (c0, n) in nts if not (hf == 0 and ti == 0)]
                rmsnorm_h(rnts, h0, 2)
                ffn(2, rnts, h0)
                rmsnorm_stats(rnts, h0)
                for ti, (c0, n) in rnts:
                    lo = c0 - h0
                    for kc in range(KC):
                        q = oq % 4
                        oq += 1
                        dve("scalar_tensor_tensor", [("s", kc, ti), ("rstd", ti), "vecs"], [("t", q)],
                            out=tt(q, n), in0=s_sb[:, kc, lo:lo + n], scalar=gcol(3, kc), in1=rstd[:, lo:lo + n], op0=ALU.mult, op1=ALU.mult)
                        dma("sp", outT[kc * 128:(kc + 1) * 128, c0 - EXT:c0 - EXT + n], tt(q, n), f"o{q}", [("t", q)], [("outT", kc, ti)])
                P.barrier()


    except _Stop:
        P.barrier()

    finals = [ent[2] for name, ent in P.dsem.items() if ent[2] is not None]

    P.finalize()
    with nc.Block() as block:
        @block.tensor
        def _(e):
            P.emit("pe", e)

        @block.scalar
        def _(e):
            P.emit("act", e)

        @block.vector
        def _(e):
            P.emit("dve", e)

        @block.gpsimd
        def _(e):
            P.emit("pool", e)

        @block.sync
        def _(e):
            P.emit("sp", e, final_waits=finals)
    return nc


_TRI = np.triu(np.ones((128, 128), np.float32))


def kernel(**inputs):
    inp = {k: np.asarray(v) for k, v in inputs.items()}
    x = inp["x"][0]
    meta = inp["meta_tokens"]
    wall = _pack_weights(inp)
    wall1 = np.ascontiguousarray(wall[:, :WSPLIT])
    wall3 = np.ascontiguousarray(wall[:, WSPLIT:])
    vecs = _pack_vecs(inp)
    cores = list(range(NCORE))
    in1 = []
    for c in cores:
        ext = meta if c == 0 else x[c * TOWN - EXT:c * TOWN]
        xs = np.concatenate([ext, x[c * TOWN:(c + 1) * TOWN]], axis=0)
        in1.append({"xT": np.ascontiguousarray(xs.T), "wall": wall1, "vecs": vecs, "tri": _TRI})
    r1 = run_bass_kernel_spmd(build_nc(1), in1, core_ids=cores).results
    in2 = []
    for j in cores:
        qt = np.concatenate([r1[r]["snd_qk"][j * 256:j * 256 + 128, :] for r in cores], axis=1)
        kt = np.concatenate([r1[r]["snd_qk"][j * 256 + 128:j * 256 + 256, :] for r in cores], axis=1)
        vs = np.zeros((128, 129, 128), dtype=qt.dtype)
        for r in cores:
            blk = r1[r]["snd_v"][j * T:(j + 1) * T]
            vs[:, 1 + 16 * r:17 + 16 * r, :] = blk[:TOWN].reshape(128, 16, 128)
            if r == 0:
                vs[0:16, 0, :] = blk[TOWN:]
        in2.append({"qt_in": np.ascontiguousarray(qt), "kt_in": np.ascontiguousarray(kt),
                    "vs_in": vs.reshape(128, 129 * 128), "vecs": vecs, "tri": _TRI})
    r2 = run_bass_kernel_spmd(build_nc(2), in2, core_ids=cores).results
    in3 = []
    for c in cores:
        yatt = np.concatenate([r2[k]["snd_y"][:, c * TOWN:(c + 1) * TOWN] for k in cores], axis=0)
        in3.append({"s1d": r1[c]["s1d"], "yatt_in": np.ascontiguousarray(yatt), "wall": wall3, "vecs": vecs, "tri": _TRI})
    r3 = run_bass_kernel_spmd(build_nc(3), in3, core_ids=cores).results
    out = np.empty((1, SEQ, D), np.float32)
    for c in cores:
        out[0, c * TOWN:(c + 1) * TOWN, :] = r3[c]["outT"].T
    return out
```
